# Optimizing a Trainium2 kernel written in Bass

```python
import math
import jax, jax.numpy as jnp
from jax import lax
import numpy as np

D_MODEL = 1024
BATCH = 32
SEQ = 2048
DEPTH = 1

GDN_HEADS = 4
GDN_DK = 128
GDN_DV = 128
GDN_CONV = 4
GDN_CHUNK = 64
SWA_Q_HEADS = 8
SWA_KV_HEADS = 2
SWA_HEAD_DIM = 64
WINDOW = 128
REL_BUCKETS = 32
REL_MAX_DIST = 128
D_FF = 2816
FFN_CONV = 3
NORM_EPS = 1e-6
NEG_INF = -1e30

GDN_QK = GDN_HEADS * GDN_DK
GDN_QKV = GDN_HEADS * (2 * GDN_DK + GDN_DV)
GDN_OUT = GDN_HEADS * GDN_DV
SWA_Q = SWA_Q_HEADS * SWA_HEAD_DIM
SWA_KV = SWA_KV_HEADS * SWA_HEAD_DIM
D_MIX = GDN_OUT + SWA_Q
D_IN = GDN_QKV + GDN_OUT + 2 * GDN_HEADS + SWA_Q + 2 * SWA_KV

kernel_name = "hybrid_gdn_swa_sink_t5bias_convffn_sandwich"


def rms_norm(x, w):
    xf = x.astype(jnp.float32)
    y = xf * lax.rsqrt(jnp.mean(xf * xf, axis=-1, keepdims=True) + NORM_EPS)
    return (y * w.astype(jnp.float32)).astype(x.dtype)


def l2_norm(x):
    return x * lax.rsqrt(jnp.sum(x * x, axis=-1, keepdims=True) + NORM_EPS)


def causal_dwconv(x, w):
    K, C = w.shape
    return lax.conv_general_dilated(
        x, w.reshape(K, 1, C).astype(x.dtype), window_strides=(1,),
        padding=[(K - 1, 0)], dimension_numbers=("NWC", "WIO", "NWC"),
        feature_group_count=C)


def gated_delta_rule_chunked(q, k, v, g, beta):
    B, T, H, dk = q.shape
    dv = v.shape[-1]
    C = GDN_CHUNK
    N = T // C

    def chunks(t):
        return jnp.moveaxis(t.reshape(B, N, C, H, *t.shape[3:]), 3, 2)

    qc = chunks(q * (dk ** -0.5))
    kc, vc, gc, bc = chunks(k), chunks(v), chunks(g), chunks(beta)
    G = jnp.cumsum(gc, axis=-1)
    causal = jnp.tril(jnp.ones((C, C), dtype=bool))
    strict = jnp.tril(jnp.ones((C, C), dtype=bool), k=-1)
    diff = G[..., :, None] - G[..., None, :]
    decay = jnp.where(causal, jnp.exp(jnp.where(causal, diff, 0.0)), 0.0)

    kb = kc * bc[..., None]
    A = jnp.where(strict, jnp.einsum("bnhid,bnhjd->bnhij", kb, kc) * decay, 0.0)
    eye = jnp.broadcast_to(jnp.eye(C, dtype=q.dtype), A.shape)
    Tinv = lax.linalg.triangular_solve(eye + A, eye, left_side=True, lower=True,
                                       unit_diagonal=True)
    u = jnp.einsum("bnhij,bnhjd->bnhid", Tinv, vc * bc[..., None])
    w = jnp.einsum("bnhij,bnhjd->bnhid", Tinv, kb * jnp.exp(G)[..., None])
    qk = jnp.einsum("bnhid,bnhjd->bnhij", qc, kc) * decay
    q_dec = qc * jnp.exp(G)[..., None]
    k_dec = kc * jnp.exp(G[..., -1:] - G)[..., None]
    g_last = jnp.exp(G[..., -1])

    xs = tuple(jnp.moveaxis(t, 1, 0) for t in (u, w, qk, q_dec, k_dec, g_last))

    def step(S, inp):
        u_n, w_n, qk_n, qd_n, kd_n, gl_n = inp
        v_new = u_n - jnp.einsum("bhck,bhkv->bhcv", w_n, S)
        o = jnp.einsum("bhck,bhkv->bhcv", qd_n, S) + jnp.einsum("bhij,bhjv->bhiv", qk_n, v_new)
        S = S * gl_n[..., None, None] + jnp.einsum("bhck,bhcv->bhkv", kd_n, v_new)
        return S, o

    S0 = jnp.zeros((B, H, dk, dv), dtype=q.dtype)
    _, o = lax.scan(step, S0, xs)
    return jnp.transpose(o, (1, 0, 3, 2, 4)).reshape(B, T, H, dv)


def t5_causal_bucket(dist):
    max_exact = REL_BUCKETS // 2
    is_small = dist < max_exact
    large = max_exact + (jnp.log(jnp.maximum(dist, 1).astype(jnp.float32) / max_exact)
                         / math.log(REL_MAX_DIST / max_exact)
                         * (REL_BUCKETS - max_exact)).astype(jnp.int32)
    large = jnp.minimum(large, REL_BUCKETS - 1)
    return jnp.where(is_small, dist, large)


def band_geometry():
    qi = jnp.arange(WINDOW, dtype=jnp.int32)[:, None]
    sj = jnp.arange(2 * WINDOW, dtype=jnp.int32)[None, :]
    dist = qi + WINDOW - sj
    in_band = (dist >= 0) & (dist < WINDOW)
    return dist, in_band


def sliding_window_attention(q, k, v, sinks, bias):
    B, T, Hq, hd = q.shape
    Hkv = k.shape[2]
    G = Hq // Hkv
    W = WINDOW
    NB = T // W
    qb = q.reshape(B, NB, W, Hkv, G, hd)

    def band(t):
        tp = jnp.pad(t, ((0, 0), (W, 0), (0, 0), (0, 0))).reshape(B, NB + 1, W, Hkv, hd)
        return jnp.concatenate([tp[:, :-1], tp[:, 1:]], axis=2)

    kb, vb = band(k), band(v)
    s = jnp.einsum("bnqkgd,bnskd->bnkgqs", qb, kb).astype(jnp.float32) * (hd ** -0.5)
    s = s + bias.reshape(Hkv, G, W, 2 * W)
    _, in_band = band_geometry()
    key_pos = (jnp.arange(NB, dtype=jnp.int32)[:, None, None] * W
               + jnp.arange(2 * W, dtype=jnp.int32)[None, None, :] - W)
    mask = in_band[None] & (key_pos >= 0)
    s = jnp.where(mask[None, :, None, None], s, NEG_INF)
    sink = sinks.astype(jnp.float32).reshape(Hkv, G)[..., None, None]
    m = jnp.maximum(jnp.max(s, axis=-1, keepdims=True), sink)
    p = jnp.exp(s - m)
    p = p / (jnp.sum(p, axis=-1, keepdims=True) + jnp.exp(sink - m))
    o = jnp.einsum("bnkgqs,bnskd->bnqkgd", p.astype(vb.dtype), vb)
    return o.reshape(B, T, Hq, hd)


def setup_inputs(seed: int = 0) -> dict:
    key = jax.random.key(seed)
    ks = jax.random.split(key, 20)
    f32 = jnp.float32

    def nrm(k, shape, scale):
        return jax.random.normal(k, shape, f32) * scale

    def gain(k, shape):
        return 1.0 + 0.02 * jax.random.normal(k, shape, f32)

    dt = jnp.exp(jax.random.uniform(ks[5], (DEPTH, GDN_HEADS), f32)
                 * (math.log(0.1) - math.log(0.001)) + math.log(0.001))
    return {
        "x": jax.random.normal(ks[0], (BATCH, SEQ, D_MODEL), f32),
        "pre_mix_norm_w": gain(ks[1], (DEPTH, D_MODEL)),
        "w_in": nrm(ks[2], (DEPTH, D_MODEL, D_IN), D_MODEL ** -0.5),
        "gdn_conv_w": nrm(ks[3], (DEPTH, GDN_CONV, GDN_QKV), GDN_CONV ** -0.5),
        "gdn_a_log": jnp.log(jax.random.uniform(ks[4], (DEPTH, GDN_HEADS), f32, 1.0, 16.0)),
        "gdn_dt_bias": dt + jnp.log(-jnp.expm1(-dt)),
        "gdn_norm_w": gain(ks[6], (DEPTH, GDN_DV)),
        "swa_sinks": nrm(ks[7], (DEPTH, SWA_Q_HEADS), 1.0),
        "rel_bias_table": nrm(ks[8], (REL_BUCKETS, SWA_Q_HEADS), 0.5),
        "swa_norm_w": gain(ks[9], (DEPTH, SWA_Q)),
        "w_out": nrm(ks[10], (DEPTH, D_MIX, D_MODEL), D_MIX ** -0.5),
        "post_mix_norm_w": gain(ks[11], (DEPTH, D_MODEL)),
        "pre_ffn_norm_w": gain(ks[12], (DEPTH, D_MODEL)),
        "w_gate": nrm(ks[13], (DEPTH, D_MODEL, D_FF), D_MODEL ** -0.5),
        "w_up": nrm(ks[14], (DEPTH, D_MODEL, D_FF), D_MODEL ** -0.5),
        "ffn_conv_w": nrm(ks[15], (DEPTH, FFN_CONV, D_FF), FFN_CONV ** -0.5),
        "ffn_conv_b": nrm(ks[16], (DEPTH, D_FF), 0.02),
        "w_down": nrm(ks[17], (DEPTH, D_FF, D_MODEL), D_FF ** -0.5),
        "post_ffn_norm_w": gain(ks[18], (DEPTH, D_MODEL)),
    }


def reference(x, pre_mix_norm_w, w_in, gdn_conv_w, gdn_a_log, gdn_dt_bias, gdn_norm_w,
              swa_sinks, rel_bias_table, swa_norm_w, w_out, post_mix_norm_w,
              pre_ffn_norm_w, w_gate, w_up, ffn_conv_w, ffn_conv_b, w_down,
              post_ffn_norm_w):
    f32 = jnp.float32
    B, T, _ = x.shape
    dist, _ = band_geometry()
    bucket = t5_causal_bucket(jnp.maximum(dist, 0))
    rel_bias = jnp.transpose(rel_bias_table.astype(f32)[bucket], (2, 0, 1))

    sizes = [GDN_QKV, GDN_OUT, GDN_HEADS, GDN_HEADS, SWA_Q, SWA_KV, SWA_KV]
    split_at = [int(s) for s in np.cumsum(sizes)[:-1]]

    for l in range(DEPTH):
        h = rms_norm(x, pre_mix_norm_w[l])
        proj = h @ w_in[l]
        qkv_g, z, a, b, q_s, k_s, v_s = jnp.split(proj, split_at, axis=-1)

        qkv_g = jax.nn.silu(causal_dwconv(qkv_g, gdn_conv_w[l]).astype(f32))
        qg, kg, vg = jnp.split(qkv_g, [GDN_QK, 2 * GDN_QK], axis=-1)
        qg = l2_norm(qg.reshape(B, T, GDN_HEADS, GDN_DK))
        kg = l2_norm(kg.reshape(B, T, GDN_HEADS, GDN_DK))
        vg = vg.reshape(B, T, GDN_HEADS, GDN_DV)
        beta = jax.nn.sigmoid(b.astype(f32))
        g = -jnp.exp(gdn_a_log[l].astype(f32)) * jax.nn.softplus(
            a.astype(f32) + gdn_dt_bias[l].astype(f32))
        o_g = gated_delta_rule_chunked(qg, kg, vg, g, beta)
        o_g = rms_norm(o_g, gdn_norm_w[l]) * jax.nn.silu(
            z.astype(f32).reshape(B, T, GDN_HEADS, GDN_DV))
        o_g = o_g.reshape(B, T, GDN_OUT).astype(x.dtype)

        o_s = sliding_window_attention(
            q_s.reshape(B, T, SWA_Q_HEADS, SWA_HEAD_DIM),
            k_s.reshape(B, T, SWA_KV_HEADS, SWA_HEAD_DIM),
            v_s.reshape(B, T, SWA_KV_HEADS, SWA_HEAD_DIM),
            swa_sinks[l], rel_bias)
        o_s = rms_norm(o_s.reshape(B, T, SWA_Q), swa_norm_w[l])

        mix = jnp.concatenate([o_g, o_s], axis=-1) @ w_out[l]
        x = x + rms_norm(mix, post_mix_norm_w[l])

        h = rms_norm(x, pre_ffn_norm_w[l])
        gate = causal_dwconv(h @ w_gate[l], ffn_conv_w[l]) + ffn_conv_b[l]
        y = (jax.nn.gelu(gate, approximate=True) * (h @ w_up[l])) @ w_down[l]
        x = x + rms_norm(y, post_ffn_norm_w[l])
    return x
```

```python
import math
from contextlib import ExitStack

import numpy as np
import concourse.bass as bass
import concourse.mybir as mybir
from concourse.bass_utils import run_bass_kernel_spmd

F32 = mybir.dt.float32
BF16 = mybir.dt.bfloat16
AF = mybir.ActivationFunctionType
ALU = mybir.AluOpType
AX = mybir.AxisListType

ENG = ("pe", "act", "dve", "pool", "sp")
NEG = -30000.0
EPS = 1e-6
NRING = 4
N_CORES = 8


class Cell:
    __slots__ = ("name", "w", "r", "sem", "semcnt", "excl")

    def __init__(self, name):
        self.name = name
        self.excl = False
        self.w = None
        self.r = {}
        self.sem = None
        self.semcnt = 0


class Op:
    __slots__ = ("eng", "fn", "deps", "sig", "sigval", "dma", "dsem", "dval")


class Sched:
    def __init__(self, nc, stack):
        self.nc = nc
        self.stack = stack
        self.ops = {e: [] for e in ENG}
        self.esem = {}
        self.stores = []
        self.ncell = 0

    def cell(self, name="c"):
        self.ncell += 1
        return Cell("%s%d" % (name, self.ncell))

    def cells(self, n, name="c"):
        return [self.cell(name) for _ in range(n)]

    def add(self, eng, fn, reads=(), writes=(), dma=None, store=False, extra=()):
        op = Op()
        op.eng = eng
        op.fn = fn
        op.sig = False
        op.sigval = 0
        op.dma = dma is not None
        op.dsem = None
        op.dval = 0
        deps = list(extra)
        rawset = set()
        for c in reads:
            if c.w is not None:
                deps.append(c.w)
                rawset.add(id(c.w))
            if c.excl:
                for k, o in c.r.items():
                    if k != eng:
                        deps.append(o)
        for c in writes:
            if c.w is not None:
                deps.append(c.w)
            deps.extend(c.r.values())
        real = []
        seen = set()
        for d in deps:
            if d is op or id(d) in seen:
                continue
            seen.add(id(d))
            if (not d.dma) and (not op.dma) and d.eng == eng:
                if eng == "pe":
                    continue
            real.append(d)
            if not d.dma:
                d.sig = True
        op.deps = real
        if op.dma:
            if dma.sem is None:
                dma.sem = self.stack.enter_context(self.nc.semaphore("d_" + dma.name))
            dma.semcnt += 16
            op.dsem = dma.sem
            op.dval = dma.semcnt
        key = ("dma", id(op)) if op.dma else eng
        for c in reads:
            c.r[key] = op
        for c in writes:
            c.w = op
            c.r = {}
        self.ops[eng].append(op)
        if store:
            self.stores.append(op)
        return op

    def emit(self):
        nc = self.nc
        if self.stores:
            fin = Op()
            fin.eng = "sp"
            fin.fn = lambda e: e.nop()
            fin.sig = False
            fin.sigval = 0
            fin.dma = False
            fin.deps = list(self.stores)
            self.ops["sp"].append(fin)
        for e in ENG:
            n = 0
            for op in self.ops[e]:
                if op.sig:
                    n += 1
                    op.sigval = n
            self.esem[e] = self.stack.enter_context(nc.semaphore("e_" + e))
        stats = {}
        with nc.Block() as block:
            decos = {"pe": block.tensor, "act": block.scalar, "dve": block.vector,
                     "pool": block.gpsimd, "sp": block.sync}
            for e in ENG:
                ops = self.ops[e]
                if not ops:
                    continue

                def body(eng, ops=ops, e=e):
                    waited = {}
                    nw = 0
                    for op in ops:
                        need = {}
                        for d in op.deps:
                            if d.dma:
                                sem, val = d.dsem, d.dval
                            else:
                                sem, val = self.esem[d.eng], d.sigval
                            k = id(sem)
                            if k not in need or need[k][1] < val:
                                need[k] = (sem, val)
                        for k, (sem, val) in need.items():
                            if waited.get(k, 0) >= val:
                                continue
                            eng.wait_ge(sem, val)
                            waited[k] = val
                            nw += 1
                        ins = op.fn(eng)
                        if op.dma:
                            ins.then_inc(op.dsem, 16)
                        elif op.sig:
                            ins.then_inc(self.esem[e], 1)
                    stats[e] = (len(ops), nw)

                decos[e](body)
        return stats


class PsumAlloc:
    def __init__(self):
        self.free = [True] * 8
        self.rel = [0] * 8
        self.clock = 0

    def alloc(self, n=1):
        best, bestt = None, None
        for b in range(0, 8 - n + 1):
            if all(self.free[b:b + n]):
                t = max(self.rel[b:b + n])
                if best is None or t < bestt:
                    best, bestt = b, t
        if best is None:
            raise RuntimeError("psum banks exhausted")
        for i in range(best, best + n):
            self.free[i] = False
        return best

    def release(self, b, n=1):
        self.clock += 1
        for i in range(b, b + n):
            assert not self.free[i]
            self.free[i] = True
            self.rel[i] = self.clock


def _const_table():
    c = np.zeros((128, 1280), np.float32)
    p = np.arange(128)[:, None]
    f = np.arange(128)[None, :]
    same = (p // 64) == (f // 64)
    c[:, 0:128] = (p == f)
    c[:, 128:256] = same & (p <= f)
    c[:, 256:384] = same
    c[:, 384:512] = (p // 64 == 0) * np.ones((1, 128))
    c[:, 512:640] = (p // 64 == 1) * np.ones((1, 128))
    c[:, 640:768] = np.where(same & (p > f), 0.0, NEG)
    c[:, 768:896] = np.where(same & (f > p), 0.0, NEG)
    c[:, 896:1024] = np.where(same & (f >= p), 0.0, NEG)
    q = np.arange(128)[:, None]
    s = np.arange(256)[None, :]
    dist = q + 128 - s
    c[:, 1024:1280] = np.where((dist >= 0) & (dist < 128), 0.0, NEG)
    return c


def _bucket_table():
    q = np.arange(128)[:, None]
    s = np.arange(256)[None, :]
    dist = np.maximum(q + 128 - s, 0)
    max_exact = 16
    large = max_exact + (np.log(np.maximum(dist, 1).astype(np.float32) / max_exact)
                         / math.log(128 / max_exact) * (32 - max_exact)).astype(np.int32)
    large = np.minimum(large, 31)
    return np.where(dist < max_exact, dist, large)


def build_program(NSEQ=4, NMC=4, taps=None, limit=99):
    T = NSEQ * NMC * 512
    nc = bass.Bass("TRN2", target_bir_lowering=False)

    def din(name, shape, dt=F32):
        return nc.dram_tensor(name, list(shape), dt, kind="ExternalInput").ap()

    x_d = din("x", [T, 1024])
    w_in_d = din("w_in", [1024, 2824])
    w_out_d = din("w_out", [1024, 1024])
    w_gate_d = din("w_gate", [1024, 2816])
    w_up_d = din("w_up", [1024, 2816])
    w_down_d = din("w_down", [2816, 1024])
    vec8_d = din("vec8", [128, 24])
    gcw_d = din("gcw", [128, 48])
    fcw_d = din("fcw", [128, 66])
    fcb_d = din("fcb", [128, 22])
    wpm_d = din("wpm", [128, 1024])
    wpf_d = din("wpf", [128, 1024])
    small_d = din("small", [128, 16])
    relb_d = din("relb", [128, 2048])
    cst_d = din("cst", [128, 1280])
    y_d = nc.dram_tensor("y", [T, 1024], F32, kind="ExternalOutput").ap()
    winF_d = nc.dram_tensor("winF", [22, 128, 8, 128], BF16, kind="Internal").ap()
    ffn_d = nc.dram_tensor("ffnw", [22, 128, 19, 128], BF16, kind="Internal").ap()
    wd_d = nc.dram_tensor("wdw", [22, 128, 1024], BF16, kind="Internal").ap()
    wo_d = nc.dram_tensor("wow", [8, 128, 1024], BF16, kind="Internal").ap()
    tap_out = {}

    st = ExitStack()
    with st:
        S = Sched(nc, st)
        PS = PsumAlloc()

        def sb(name, shape, dt):
            return st.enter_context(nc.sbuf_tensor("s_" + name, list(shape), dt))

        def _c(*a, **k):
            return (a, k)

        def A(eng, fn, ca=None, r=(), w=(), **kw):
            if not callable(fn):
                if isinstance(fn, tuple):
                    fn, ca = fn
                meth, (pa, ka) = fn, ca
                fn = lambda e: getattr(e, meth)(*pa, **ka)
            return S.add(eng, fn, reads=r, writes=w, **kw)

        cstf = sb("cstf", [128, 1280], F32)
        identb = sb("identb", [128, 128], BF16)
        i4b = sb("i4b", [128, 4, 128], BF16)
        negm = sb("negm", [128, 3, 512], BF16)
        onesc = sb("onesc", [128, 2], BF16)
        vec8 = sb("vec8", [128, 24], F32)
        gcw = sb("gcw", [128, 48], F32)
        fcw = sb("fcw", [128, 66], F32)
        fcb = sb("fcb", [128, 22], F32)
        wpm = sb("wpm", [128, 1024], F32)
        wpf = sb("wpf", [128, 1024], F32)
        small = sb("small", [128, 16], F32)
        negA = sb("negA", [128, 4], F32)
        biasb = sb("biasb", [128, 8, 256], F32)
        dg = sb("dg", [128, 48, 128], BF16)
        WinT = sb("WinT", [128, 8, 136], BF16)
        xb = [sb("xb%d" % b, [128, 1024], F32) for b in range(4)]
        hT = sb("hT", [128, 8, 512], BF16)
        junk = sb("junk", [128, 4, 128], BF16)
        hnB = [sb("hn%d" % i, [128, 1024], BF16) for i in range(4)]
        sttB = [sb("stt%d" % i, [128, 8], F32) for i in range(4)]
        pre = [sb("pre%d" % i, [128, 516], BF16) for i in range(2)]
        ghg = sb("ghg", [128, 12, 4], BF16)
        qkvT = sb("qkvT", [128, 12, 512], BF16)
        szT = sb("szT", [128, 4, 512], BF16)
        qsT = sb("qsT", [128, 4, 512], BF16)
        ksT = sb("ksT", [128, 2, 640], BF16)
        vs = sb("vs", [128, 5, 128], BF16)
        ab = sb("ab", [128, 4, 8], F32)
        actT = sb("actT", [128, 22, 512], BF16)
        sq = sb("sq", [128, 8, 128], BF16)
        ts = sb("ts", [128, 352], F32)
        tsh = sb("tsh", [128, 48], BF16)
        tsl = sb("tsl", [128, 48], BF16)
        tsr = sb("tsr", [128, 48], F32)
        Et = [sb("E%d" % i, [128, 4, 128], F32) for i in range(4)]
        Pn = [sb("Pn%d" % i, [128, 4, 128], BF16) for i in range(2)]
        PTt = [sb("PT%d" % i, [128, 4, 128], BF16) for i in range(2)]
        Mt = [sb("M%d" % i, [128, 4, 128], BF16) for i in range(2)]
        TinvB = [sb("Tinv%d" % i, [128, 4, 128], BF16) for i in range(2)]
        QKTB = [sb("QKT%d" % i, [128, 4, 128], BF16) for i in range(2)]
        qdTB = [sb("qdT%d" % i, [128, 4, 128], BF16) for i in range(2)]
        kbgB = [sb("kbg%d" % i, [128, 4, 128], BF16) for i in range(2)]
        kdecB = [sb("kdec%d" % i, [128, 4, 128], BF16) for i in range(2)]
        vbB = [sb("vb%d" % i, [128, 4, 128], BF16) for i in range(2)]
        nwTB = [sb("nwT%d" % i, [128, 4, 128], BF16) for i in range(2)]
        vnew = sb("vnew", [128, 4, 128], BF16)
        o_sb = sb("o_sb", [128, 4, 128], F32)
        on = sb("on", [128, 4, 128], BF16)
        Sf = sb("Sf", [128, 4, 128], F32)
        Sb = sb("Sb", [128, 4, 128], BF16)
        gst = sb("gst", [128, 16], F32)
        sc_sb = sb("sc_sb", [128, 8, 256], F32)
        pt = sb("pt", [128, 8, 256], BF16)
        pT_sb = sb("pT_sb", [128, 16, 128], BF16)
        os_sb = sb("os_sb", [128, 8, 64], F32)
        osn = sb("osn", [128, 512], BF16)
        sst = sb("sst", [128, 64], F32)
        ssum = sb("ssum", [128, 8], F32)
        sden = sb("sden", [128, 8], F32)
        gpre = [sb("gpre%d" % i, [128, 516], BF16) for i in range(2)]
        ghf = sb("ghf", [128, 22, 2], BF16)
        ge = [sb("ge%d" % i, [128, 512], BF16) for i in range(2)]
        ring = [sb("ring%d" % i, [128, 19, 128], BF16) for i in range(NRING)]
        NSTG = 8
        actT_f32 = actT[:].rearrange("p f t -> p (f t)").bitcast(F32)
        scsb_b16 = sc_sb[:].rearrange("p h s -> p (h s)").bitcast(BF16)
        stg = [actT_f32[:, i * 1024:(i + 1) * 1024] for i in range(4)] + [xb[i][:] for i in range(4)]
        cvt = [scsb_b16[:, i * 1024:(i + 1) * 1024] for i in range(4)] + [hnB[i][:] for i in range(4)]
        ps = st.enter_context(nc.psum_tensor("ps", [128, 4096], F32))

        def PB(b, n=1):
            return ps[:, b * 512:(b + n) * 512]

        def PBb(b, n=1):
            return ps[:, b * 512:(b + n) * 512].bitcast(BF16)

        cC = S.cell("const")
        cIdb, cI4, cNegm, cOnes, cNegA, cBias, cDg, cWinT = S.cells(8, "k")
        cx = S.cells(4, "x")
        chT = S.cells(4, "hT")
        chnB = S.cells(4, "hn")
        csttB = S.cells(4, "stt")
        cjk = S.cells(4, "jk")
        cpreh = S.cells(2, "preh")
        cpre = S.cells(2, "pre")
        cghg, cghf = S.cells(2, "gh")
        cqkv = S.cells(12, "qkv")
        csz, cqs, cks, cab = S.cells(4, "m")
        cvs = S.cells(5, "vs")
        cact = S.cells(22, "act")
        csq, cts, ctsh = S.cells(3, "g")
        cE = [S.cells(4, "E") for _ in range(4)]
        cPn = S.cells(2, "Pn")
        cPT = S.cells(2, "PT")
        cM = S.cells(2, "M")
        cvnew, co, con, cSb, cgst = S.cells(5, "gd")
        cTinvB, cQKTB, cqdB, ckbgB, ckdecB, cvbB, cnwB = [S.cells(2, "gp") for _ in range(7)]
        cSf = S.cells(4, "Sf")
        cgsth = S.cells(4, "gsth")
        csc, cpT, cos, cosn, csst, cden, csn, cpT2 = S.cells(8, "sw")
        cpt = S.cells(8, "pt")
        csum = S.cells(8, "sum")
        cgpre = S.cells(2, "gpre")
        cgpreh = S.cells(2, "gpreh")
        cge = S.cells(2, "ge")
        cring = S.cells(NRING, "ring")
        cstg = S.cells(8, "stg")
        ccvt = S.cells(8, "cvt")
        cps = S.cells(8, "ps")
        for c_ in cps:
            c_.excl = True
        cscr = S.cell("scr")

        def pcells(b, n=1):
            return cps[b:b + n]

        for (tl, src) in ((cstf, cst_d), (vec8, vec8_d), (gcw, gcw_d), (fcw, fcw_d), (fcb, fcb_d), (wpm, wpm_d),
                          (wpf, wpf_d), (small, small_d), (biasb[:].rearrange("p h s -> p (h s)"), relb_d)):
            t_ap = tl[:] if not isinstance(tl, bass.AP) else tl
            A("sp", "dma_start", _c(out=t_ap, in_=src), w=[cC], dma=cC)
        identf = cstf[:, 0:128]
        A("dve", "tensor_copy", _c(identb[:], identf), r=[cC], w=[cIdb])
        A("dve", "tensor_copy", _c(i4b[:], identf.unsqueeze(1).to_broadcast([128, 4, 128])), r=[cC], w=[cI4])
        for k_ in range(3):
            A("dve", "tensor_copy", _c(negm[:, k_, :].rearrange("p (h t) -> p h t", h=4),
                                       cstf[:, 640 + 128 * k_:768 + 128 * k_].unsqueeze(1).to_broadcast([128, 4, 128])), r=[cC], w=[cNegm])
        A("pool", "memset", _c(onesc[:], 1.0), w=[cOnes])
        A("pool", "memset", _c(ts[:], 0.0), w=[cts])
        A("act", "activation", _c(out=negA[:], in_=small[:, 12:16], func=AF.Exp), r=[cC], w=[cNegA])
        A("dve", "tensor_scalar", _c(out=negA[:], in0=negA[:], scalar1=-1.0, scalar2=None, op0=ALU.mult), r=[cNegA], w=[cNegA])
        A("pool", "tensor_tensor", _c(out=biasb[:], in0=biasb[:], in1=cstf[:, 1024:1280].unsqueeze(1).to_broadcast([128, 8, 256]),
                                            op=ALU.add), r=[cC], w=[cBias])
        for j in range(48):
            A("dve" if j % 2 else "pool",
              "tensor_scalar", _c(out=dg[:, j, :], in0=identf, scalar1=gcw[:, j:j + 1], scalar2=None, op0=ALU.mult),
              r=[cC], w=[cDg])

        cast_engs = ("dve", "act")
        _STQ = "act"
        _LDQ = ["sp"]
        store_ops = []
        jobs = []

        def convert(src, kc, c0, cw, scale_col, dst_fn, rows=None):
            i = len(jobs)
            sl = i % NSTG
            eng = cast_engs[i % len(cast_engs)]
            job = [[], [], []]
            jobs.append(job)
            if rows is None:
                job[0].append(lambda: A(_LDQ[i % len(_LDQ)], "dma_start", _c(out=stg[sl][:, 0:cw], in_=src[kc * 128:(kc + 1) * 128, c0:c0 + cw]), w=[cstg[sl]], dma=cstg[sl]))
            else:
                for hh, r0 in enumerate(rows):
                    job[0].append(lambda hh=hh, r0=r0: A("sp", "dma_start", _c(out=stg[sl][hh * 64:(hh + 1) * 64, 0:cw], in_=src[r0:r0 + 64, c0:c0 + cw]),
                                                         w=[cstg[sl]], dma=cstg[sl]))
            out_ap, out_cells, direct = dst_fn(sl)
            if eng == "act":
                if scale_col is None:
                    f = "activation", _c(out=out_ap, in_=stg[sl][:, 0:cw], func=AF.Copy)
                else:
                    f = "activation", _c(out=out_ap, in_=stg[sl][:, 0:cw], func=AF.Copy, scale=scale_col)
            else:
                if scale_col is None:
                    f = "tensor_copy", _c(out_ap, stg[sl][:, 0:cw])
                else:
                    f = "tensor_scalar", _c(out=out_ap, in0=stg[sl][:, 0:cw], scalar1=scale_col, scalar2=None, op0=ALU.mult)
            job[1].append(lambda: A(eng, f, r=[cstg[sl], cC], w=out_cells))
            return sl

        def store(sl, dst_ap, src_ap):
            jobs[-1][2].append(lambda: store_ops.append(A(_STQ, "dma_start", _c(out=dst_ap, in_=src_ap), r=[ccvt[sl]], dma=ccvt[sl])))

        def to_cvt(cw):
            return lambda sl: (cvt[sl][:, 0:cw], [ccvt[sl]], False)

        for kc in range(8):
            sc = vec8[:, kc:kc + 1]
            for c0 in (0, 1024):
                sl = convert(w_in_d, kc, c0, 1024, sc, to_cvt(1024))
                m0 = c0 // 128
                store(sl, winF_d[m0:m0 + 8, :, kc, :].rearrange("f p m -> p f m"), cvt[sl][:, 0:1024].rearrange("p (f m) -> p f m", f=8))
            sl = convert(w_in_d, kc, 2056, 512, sc, to_cvt(512))
            store(sl, winF_d[16:20, :, kc, :].rearrange("f p m -> p f m"), cvt[sl][:, 0:512].rearrange("p (f m) -> p f m", f=4))
            sl = convert(w_in_d, kc, 2568, 128, sc, to_cvt(128))
            for kv in range(2):
                for half in range(2):
                    store(sl, winF_d[20 + kv, :, kc, half * 64:(half + 1) * 64], cvt[sl][:, kv * 64:(kv + 1) * 64])
            convert(w_in_d, kc, 2048, 8, sc, lambda sl, kc=kc: (WinT[:, kc, 128:136], [cWinT], True))
            convert(w_in_d, kc, 2696, 128, sc, lambda sl, kc=kc: (WinT[:, kc, 0:128], [cWinT], True))
        for (wsrc, off) in ((w_gate_d, 0), (w_up_d, 8)):
            for kc in range(8):
                sc = vec8[:, 8 + kc:9 + kc]
                for c0 in (0, 1024, 2048):
                    cw = min(1024, 2816 - c0)
                    nf = cw // 128
                    sl = convert(wsrc, kc, c0, cw, sc, to_cvt(cw))
                    f0 = c0 // 128
                    store(sl, ffn_d[f0:f0 + nf, :, off + kc, :].rearrange("f p m -> p f m"),
                          cvt[sl][:, 0:cw].rearrange("p (f m) -> p f m", f=nf))
        for kc in range(8):
            if kc < 4:
                rows = None
            else:
                p0, p1 = 2 * (kc - 4), 2 * (kc - 4) + 1
                rows = [512 + ((p % 4) * 2 + p // 4) * 64 for p in (p0, p1)]
            sl = convert(w_out_d, kc, 0, 1024, vec8[:, 16 + kc:17 + kc], to_cvt(1024), rows=rows)
            store(sl, wo_d[kc], cvt[sl][:, 0:1024])
        for f in range(22):
            sl = convert(w_down_d, f, 0, 1024, None, to_cvt(1024))
            store(sl, wd_d[f], cvt[sl][:, 0:1024])
        for f in range(22):
            sl = len(jobs) % NSTG
            job = [[], [], []]
            jobs.append(job)
            for t in range(3):
                job[1].append(lambda f=f, t=t, sl=sl: A("dve" if t % 2 else "pool", "tensor_scalar",
                                                        _c(out=cvt[sl][:, t * 128:(t + 1) * 128], in0=identf, scalar1=fcw[:, f * 3 + t:f * 3 + t + 1],
                                                           scalar2=None, op0=ALU.mult), r=[cC], w=[ccvt[sl]]))
            store(sl, ffn_d[f, :, 16:19, :], cvt[sl][:, 0:384].rearrange("p (t m) -> p t m", t=3))
        DEPTH = NSTG - 1
        for i in range(len(jobs) + DEPTH):
            if i < len(jobs):
                for th in jobs[i][0]:
                    th()
            if i - DEPTH >= 0:
                for th in jobs[i - DEPTH][1]:
                    th()
                for th in jobs[i - DEPTH][2]:
                    th()
        last_store = {}
        for op_ in store_ops:
            last_store[id(op_.dsem)] = op_
        bar = A("sp", "nop", _c(), w=[cscr], extra=list(last_store.values()))
        csc.w = bar
        for c_ in cact + cx + chnB:
            c_.w = bar

        def items():
            for s in range(NSEQ):
                for mc in range(NMC):
                    for m in range(22):
                        yield ("in", m)
                    for k in range(8):
                        yield ("wo", k)
                    for f in range(22):
                        yield ("ffn", f)
                    for f in range(22):
                        yield ("wd", f)

        item_list = list(items())
        wstate = {"issued": 0, "next": 0}

        def issue_load(j):
            kind, idx = item_list[j]
            sl = j % NRING
            flat = ring[sl][:].rearrange("p a b -> p (a b)")
            if kind == "in":
                f = "dma_start", _c(out=ring[sl][:, 0:8, :], in_=winF_d[idx])
            elif kind == "wo":
                f = "dma_start", _c(out=flat[:, 0:1024], in_=wo_d[idx])
            elif kind == "ffn":
                f = "dma_start", _c(out=ring[sl][:], in_=ffn_d[idx])
            else:
                f = "dma_start", _c(out=flat[:, 0:1024], in_=wd_d[idx])
            A("sp", f, r=[cscr], w=[cring[sl]], dma=cring[sl])

        def wget(kind, idx):
            j = wstate["next"]
            assert item_list[j] == (kind, idx), (item_list[j], kind, idx)
            lim = min(j + NRING - 1, len(item_list) - 1)
            while wstate["issued"] <= lim:
                issue_load(wstate["issued"])
                wstate["issued"] += 1
            wstate["next"] += 1
            sl = j % NRING
            return ring[sl], cring[sl]

        LN_S = -0.5 * math.log(128.0)

        def rstd_from(ss_ap, out_ap, n, cells):
            A("act", "activation", _c(out=out_ap, in_=ss_ap, func=AF.Ln, bias=EPS, scale=1.0 / n), r=cells, w=cells[0:1])
            A("act", "activation", _c(out=out_ap, in_=out_ap, func=AF.Exp, scale=-0.5), r=cells[0:1], w=cells[0:1])

        def tap(name, ap, cells, shape, dt=F32):
            if taps is None or name not in taps:
                return
            d = nc.dram_tensor("tap_" + name, list(shape), dt, kind="ExternalOutput").ap()
            tap_out[name] = d
            A("sp", "dma_start", _c(out=d, in_=ap), r=cells, dma=cells[0], store=True)

        def reset_state():
            A("pool", "memset", _c(Sf[:], 0.0), w=cSf)
            A("pool", "memset", _c(Sb[:], 0.0), w=[cSb])
            A("pool", "memset", _c(ghg[:], 0.0), w=[cghg])
            A("pool", "memset", _c(ghf[:], 0.0), w=[cghf])
            A("pool", "memset", _c(ksT[:, :, 0:128], 0.0), w=[cks])
            A("pool", "memset", _c(vs[:, 0, :], 0.0), w=[cvs[0]])

        def drive(*gens):
            items = []
            for g in gens:
                if g is None:
                    continue
                items.append(list(g) if isinstance(g, tuple) else [g, 1])
            while items:
                for it in list(items):
                    for _ in range(it[1]):
                        try:
                            next(it[0])
                        except StopIteration:
                            items.remove(it)
                            break

        def seq(*gs):
            for g in gs:
                yield from g

        def norm_transpose(b):
            hn, chn, stt, cstt = hnB[b], chnB[b], sttB[b], csttB[b]
            A("act", "activation", _c(out=hn[:], in_=xb[b][:], func=AF.Square, accum_out=stt[:, 0:1]), r=[cx[b]], w=[chn, cstt])
            yield
            A("act", "activation", _c(out=stt[:, 1:2], in_=stt[:, 0:1], func=AF.Ln, bias=EPS, scale=1.0 / 1024), r=[cstt], w=[cstt])
            yield
            A("act", "activation", _c(out=stt[:, 1:2], in_=stt[:, 1:2], func=AF.Exp, scale=-0.5), r=[cstt], w=[cstt])
            yield
            A("act", "activation", _c(out=hn[:], in_=xb[b][:], func=AF.Copy, scale=stt[:, 1:2]), r=[cx[b], cstt], w=[chn])
            yield
            bk = PS.alloc()
            for kc in range(8):
                A("pe", "transpose", _c(PBb(bk)[:, kc * 128:(kc + 1) * 128], hn[:, kc * 128:(kc + 1) * 128], identb[:]),
                  r=[chn, cIdb], w=pcells(bk))
            yield
            A("dve", "tensor_copy", _c(hT[:, :, b * 128:(b + 1) * 128], PBb(bk).rearrange("p (k t) -> p k t", k=8)),
              r=pcells(bk), w=[chT[b]])
            PS.release(bk)
            yield

        def bail(tok0):
            for b in range(4):
                A("sp", "dma_start", _c(out=y_d[tok0 + b * 128: tok0 + (b + 1) * 128, :], in_=xb[b][:]), r=[cx[b]], dma=cx[b], store=True)

        def macro_chunk(s, mc):
            tok0 = (s * NMC + mc) * 512
            first = (mc == 0)
            if limit < 1:
                for b in range(4):
                    A("sp", "dma_start", _c(out=xb[b][:], in_=x_d[tok0 + b * 128: tok0 + (b + 1) * 128, :]), w=[cx[b]], dma=cx[b])
                return bail(tok0)
            for b in range(4):
                A("sp", "dma_start", _c(out=xb[b][:], in_=x_d[tok0 + b * 128: tok0 + (b + 1) * 128, :]), w=[cx[b]], dma=cx[b])
            drive(*[norm_transpose(b) for b in range(4)])
            if limit < 2:
                return bail(tok0)
            for b in range(4):
                bk = PS.alloc()
                for kc in range(8):
                    A("pe", "matmul", _c(PB(bk)[:, 0:136], lhsT=hT[:, kc, b * 128:(b + 1) * 128], rhs=WinT[:, kc, :],
                                         start=(kc == 0), stop=(kc == 7)), r=[chT[b], cWinT], w=pcells(bk))
                A("dve", "tensor_copy", _c(ab[:, b, :], PB(bk)[:, 128:136]), r=pcells(bk), w=[cab])
                A("dve", "tensor_copy", _c(vs[:, b + 1, :], PB(bk)[:, 0:128]), r=pcells(bk), w=[cvs[b + 1]])
                PS.release(bk)

            def inproj_conv(m):
                pr = m % 2
                b2 = PS.alloc()
                for i in range(4):
                    A("pe", "matmul", _c(PB(b2), lhsT=dg[:, m * 4 + i, :], rhs=pre[pr][:, i:i + 512],
                                         start=(i == 0), stop=(i == 3)), r=[cDg, cpre[pr], cpreh[pr]], w=pcells(b2))
                A("act", "activation", _c(out=qkvT[:, m, :], in_=PB(b2), func=AF.Silu), r=pcells(b2), w=[cqkv[m]])
                PS.release(b2)

            pend = {"m": None}

            def inproj_gen(m_lo, m_hi):
                for m in range(m_lo, m_hi):
                    slot, cs = wget("in", m)
                    bk = PS.alloc()
                    for kc in range(8):
                        A("pe", "matmul", _c(PB(bk), lhsT=slot[:, kc, :], rhs=hT[:, kc, :], start=(kc == 0), stop=(kc == 7)),
                          r=[cs] + chT, w=pcells(bk))
                    if m < 12:
                        pr = m % 2
                        A("pool", "tensor_copy", _c(pre[pr][:, 0:3], ghg[:, m, 0:3]), r=[cghg], w=[cpreh[pr]])
                        A("act", "activation", _c(out=pre[pr][:, 3:515], in_=PB(bk), func=AF.Copy), r=pcells(bk), w=[cpre[pr]])
                        A("pool", "tensor_copy", _c(ghg[:, m, 0:3], pre[pr][:, 512:515]), r=[cpre[pr]], w=[cghg])
                    elif m < 16:
                        A("act", "activation", _c(out=szT[:, m - 12, :], in_=PB(bk), func=AF.Silu), r=pcells(bk), w=[csz])
                    elif m < 20:
                        A("act", "activation", _c(out=qsT[:, m - 16, :], in_=PB(bk), func=AF.Copy, scale=0.125), r=pcells(bk), w=[cqs])
                    else:
                        A("dve", "tensor_copy", _c(ksT[:, m - 20, 128:640], PB(bk)), r=pcells(bk), w=[cks])
                    PS.release(bk)
                    if pend["m"] is not None:
                        inproj_conv(pend["m"])
                    pend["m"] = m if m < 12 else None
                    yield

            v44 = lambda ap: ap.rearrange("p (b h) -> p b h", b=4)
            T1, Gg, T2 = ts[:, 0:16], ts[:, 16:32], ts[:, 32:48]
            SS = ts[:, 48:80]
            G_, GL_, GLb_ = ts[:, 80:96], ts[:, 96:112], ts[:, 112:144]
            ROWA, CA3 = ts[:, 144:160], ts[:, 176:192]
            XARG, YEXP = ts[:, 192:272], ts[:, 272:352]
            KDA, CA, NLB = ts[:, 192:208], ts[:, 208:224], ts[:, 224:240]
            SS3 = SS.rearrange("p (b j) -> p b j", b=4)
            LRQ, LRK = SS3[:, :, 0:4], SS3[:, :, 4:8]
            c_ts = [cts]
            KD, KG, BETA, EGL = YEXP[:, 0:16], YEXP[:, 16:32], YEXP[:, 32:48], YEXP[:, 48:80]

            def d0_gen():
                A("dve", "tensor_tensor", _c(out=v44(T1), in0=ab[:, :, 0:4], in1=small[:, 8:12].unsqueeze(1).to_broadcast([128, 4, 4]), op=ALU.add),
                  r=[cab, cC], w=c_ts)
                yield
                A("act", "activation", _c(out=T1, in_=T1, func=AF.Exp), r=c_ts, w=c_ts)
                yield
                A("act", "activation", _c(out=T1, in_=T1, func=AF.Ln, bias=1.0), r=c_ts, w=c_ts)
                yield
                A("dve", "tensor_tensor", _c(out=v44(Gg), in0=v44(T1), in1=negA[:].unsqueeze(1).to_broadcast([128, 4, 4]), op=ALU.mult),
                  r=c_ts + [cNegA], w=c_ts)
                A("act", "activation", _c(out=v44(T2), in_=ab[:, :, 4:8], func=AF.Exp, scale=-1.0), r=[cab] + c_ts, w=c_ts)
                yield
                A("act", "activation", _c(out=T2, in_=T2, func=AF.Ln, bias=1.0), r=c_ts, w=c_ts)
                yield
                bk = PS.alloc()
                for b in range(4):
                    A("pool", "tensor_tensor", _c(out=sq[:], in0=qkvT[:, 0:8, b * 128:(b + 1) * 128], in1=qkvT[:, 0:8, b * 128:(b + 1) * 128], op=ALU.mult),
                      r=cqkv[0:8], w=[csq])
                    for j in range(8):
                        A("pe", "matmul", _c(PB(bk)[:, b * 8 + j:b * 8 + j + 1], lhsT=sq[:, j, :],
                                             rhs=onesc[:, 0:1], start=True, stop=True), r=[csq, cOnes], w=pcells(bk))
                    yield
                A("act", "activation", _c(out=SS, in_=PB(bk)[:, 0:32], func=AF.Ln, bias=EPS), r=pcells(bk) + c_ts, w=c_ts)
                PS.release(bk)
                yield
                A("dve", "tensor_scalar", _c(out=SS, in0=SS, scalar1=-0.5, scalar2=None, op0=ALU.mult), r=c_ts, w=c_ts)
                bk = PS.alloc()
                for k, off in ((0, 128), (1, 256), (2, 384), (3, 512)):
                    A("pe", "matmul", _c(PB(bk)[:, k * 16:(k + 1) * 16], lhsT=cstf[:, off:off + 128], rhs=Gg, start=True, stop=True),
                      r=c_ts + [cC], w=pcells(bk))
                yield
                A("dve", "tensor_copy", _c(ts[:, 80:144], PB(bk)[:, 0:64]), r=pcells(bk) + c_ts, w=c_ts)
                PS.release(bk)
                yield
                A("dve", "tensor_tensor", _c(out=v44(ROWA), in0=LRK, in1=v44(G_), op=ALU.subtract), r=c_ts, w=c_ts)
                A("dve", "tensor_tensor", _c(out=v44(CA), in0=LRK, in1=v44(G_), op=ALU.add), r=c_ts, w=c_ts)
                yield
                A("dve", "tensor_tensor", _c(out=CA, in0=CA, in1=T2, op=ALU.subtract), r=c_ts, w=c_ts)
                A("dve", "scalar_tensor_tensor", _c(out=v44(CA3), in0=v44(G_), scalar=LN_S, in1=LRQ, op0=ALU.add, op1=ALU.add), r=c_ts, w=c_ts)
                yield
                A("dve", "tensor_tensor", _c(out=KDA, in0=GL_, in1=ROWA, op=ALU.add), r=c_ts, w=c_ts)
                A("dve", "tensor_scalar", _c(out=NLB, in0=T2, scalar1=-1.0, scalar2=None, op0=ALU.mult), r=c_ts, w=c_ts)
                yield
                for gi, V in enumerate((ROWA, CA, CA3)):
                    gs = slice(gi * 16, (gi + 1) * 16)
                    A("dve", "tensor_copy", _c(tsh[:, gs], V), r=c_ts, w=[ctsh])
                    A("dve", "tensor_tensor", _c(out=tsr[:, gs], in0=V, in1=tsh[:, gs], op=ALU.subtract), r=c_ts + [ctsh], w=[ctsh])
                    A("dve", "tensor_copy", _c(tsl[:, gs], tsr[:, gs]), r=[ctsh], w=[ctsh])
                yield
                A("act", "activation", _c(out=YEXP[:, 0:48], in_=XARG[:, 0:48], func=AF.Exp), r=c_ts, w=c_ts)
                A("act", "activation", _c(out=YEXP[:, 48:80], in_=GLb_, func=AF.Exp), r=c_ts, w=c_ts)
                tap("ts", ts[:], c_ts, [128, 352])
                yield

            def bc4(ap16, b):
                return ap16[:, b * 4:(b + 1) * 4].unsqueeze(2).to_broadcast([128, 4, 128])

            if limit < 4:
                return bail(tok0)
            def gdn_chain(b, par):
                bs = slice(b * 128, (b + 1) * 128)
                Tinv, cTinv = TinvB[par], cTinvB[par]
                QKT, cQKT = QKTB[par], cQKTB[par]
                qdT, cqd = qdTB[par], cqdB[par]
                kbg, ckbg = kbgB[par], ckbgB[par]
                kdec, ckdec = kdecB[par], ckdecB[par]
                vb, cvb = vbB[par], cvbB[par]
                nwT, cnw = nwTB[par], cnwB[par]
                specs = ((0, 0, CA, 0), (1, 1, ROWA, 1), (2, 2, ROWA, 2), (2, None, None, 3))
                for (gi, mk, biasv, ei) in specs:
                    bk = PS.alloc()
                    if mk is not None:
                        A("pe", "matmul", _c(PB(bk), lhsT=identb[:], rhs=negm[:, mk, :], start=True, stop=False), r=[cIdb, cNegm], w=pcells(bk))
                    for h in range(4):
                        col = gi * 16 + b * 4 + h
                        A("pe", "matmul", _c(
                            PB(bk)[:, h * 128:(h + 1) * 128], lhsT=tsh[:, col:col + 1].to_broadcast([128, 128]), rhs=identb[:],
                            start=(mk is None), stop=False), r=[ctsh, cIdb], w=pcells(bk))
                        A("pe", "matmul", _c(
                            PB(bk)[:, h * 128:(h + 1) * 128], lhsT=tsl[:, col:col + 1].to_broadcast([128, 128]), rhs=identb[:],
                            start=False, stop=(mk is None or h == 3)), r=[ctsh, cIdb], w=pcells(bk))
                    if biasv is None:
                        A("act", "activation", _c(out=Et[ei][:].rearrange("p h t -> p (h t)"), in_=PB(bk), func=AF.Exp),
                          r=pcells(bk), w=cE[ei])
                    else:
                        for h in range(4):
                            col = b * 4 + h
                            A("act", "activation", _c(
                                out=Et[ei][:, h, :], in_=PB(bk)[:, h * 128:(h + 1) * 128], func=AF.Exp, bias=biasv[:, col:col + 1]),
                              r=pcells(bk) + c_ts, w=[cE[ei][h]])
                    PS.release(bk)
                    yield
                bk = PS.alloc()
                for h in range(4):
                    A("pe", "matmul", _c(PB(bk)[:, h * 128:(h + 1) * 128], lhsT=qkvT[:, 4 + h, bs], rhs=qkvT[:, 4 + h, bs],
                                         start=True, stop=True), r=[cqkv[4 + h]], w=pcells(bk))
                kk3 = PB(bk).rearrange("p (h t) -> p h t", h=4)
                A("dve", "scalar_tensor_tensor", _c(out=Pn[0][:], in0=kk3, scalar=-1.0, in1=Et[1][:], op0=ALU.mult, op1=ALU.mult),
                  r=pcells(bk) + cE[1], w=[cPn[0]])
                A("dve", "scalar_tensor_tensor", _c(out=PTt[0][:], in0=kk3, scalar=-1.0, in1=Et[0][:], op0=ALU.mult, op1=ALU.mult),
                  r=pcells(bk) + cE[0], w=[cPT[0]])
                PS.release(bk)
                yield
                bk = PS.alloc()
                for h in range(4):
                    A("pe", "matmul", _c(PB(bk)[:, h * 128:(h + 1) * 128], lhsT=qkvT[:, 4 + h, bs], rhs=qkvT[:, h, bs],
                                         start=True, stop=True), r=[cqkv[4 + h], cqkv[h]], w=pcells(bk))
                kq3 = PB(bk).rearrange("p (h t) -> p h t", h=4)
                A("dve", "tensor_tensor", _c(out=QKT[:], in0=kq3, in1=Et[2][:], op=ALU.mult), r=pcells(bk) + cE[2], w=[cQKT])
                PS.release(bk)
                A("pool", "tensor_tensor", _c(out=Mt[0][:], in0=Pn[0][:], in1=i4b[:], op=ALU.add), r=[cPn[0], cI4], w=[cM[0]])
                A("pool", "tensor_tensor", _c(out=qdT[:], in0=qkvT[:, 0:4, bs], in1=Et[3][:], op=ALU.mult), r=cqkv[0:4] + cE[3], w=[cqd])
                yield
                def m_update(k):
                    cur, nxt = (k - 1) % 2, k % 2
                    bk3 = PS.alloc()
                    for h in range(4):
                        A("pe", "matmul", _c(PB(bk3)[:, h * 128:(h + 1) * 128], lhsT=PTt[nxt][:, h, :], rhs=Mt[cur][:, h, :],
                                             start=True, stop=True), r=[cPT[nxt], cM[cur]], w=pcells(bk3))
                    mdst, cmdst = (Tinv, cTinv) if k == 5 else (Mt[nxt], cM[nxt])
                    A("dve", "tensor_tensor", _c(out=mdst[:].rearrange("p h t -> p (h t)"), in0=PB(bk3),
                                                 in1=Mt[cur][:].rearrange("p h t -> p (h t)"), op=ALU.add), r=pcells(bk3) + [cM[cur]], w=[cmdst])
                    PS.release(bk3)

                for k in range(1, 6):
                    cur, nxt = (k - 1) % 2, k % 2
                    bk2 = PS.alloc()
                    for h in range(4):
                        A("pe", "matmul", _c(PB(bk2)[:, h * 128:(h + 1) * 128], lhsT=Pn[cur][:, h, :], rhs=PTt[cur][:, h, :],
                                             start=True, stop=True), r=[cPT[cur], cPn[cur]], w=pcells(bk2))
                    if k < 5:
                        bk = PS.alloc()
                        for h in range(4):
                            A("pe", "matmul", _c(PB(bk)[:, h * 128:(h + 1) * 128], lhsT=PTt[cur][:, h, :], rhs=Pn[cur][:, h, :],
                                                 start=True, stop=True), r=[cPT[cur], cPn[cur]], w=pcells(bk))
                    if k > 1:
                        m_update(k - 1)
                    A("dve", "tensor_copy", _c(PTt[nxt][:].rearrange("p h t -> p (h t)"), PB(bk2)), r=pcells(bk2), w=[cPT[nxt]])
                    PS.release(bk2)
                    if k < 5:
                        A("act", "activation", _c(out=Pn[nxt][:].rearrange("p h t -> p (h t)"), in_=PB(bk), func=AF.Copy),
                          r=pcells(bk), w=[cPn[nxt]])
                        PS.release(bk)
                    yield
                m_update(5)
                yield
                bk = PS.alloc()
                for h in range(4):
                    A("pe", "transpose", _c(PBb(bk)[:, h * 128:(h + 1) * 128], qkvT[:, 4 + h, bs], identb[:]),
                      r=[cqkv[4 + h], cIdb], w=pcells(bk))
                kt3 = PBb(bk)[:, 0:512].rearrange("p (h t) -> p h t", h=4)
                A("dve", "tensor_tensor", _c(out=kbg[:], in0=kt3, in1=bc4(KG, b), op=ALU.mult), r=pcells(bk) + c_ts, w=[ckbg])
                A("dve", "tensor_tensor", _c(out=kdec[:], in0=kt3, in1=bc4(KD, b), op=ALU.mult), r=pcells(bk) + c_ts, w=[ckdec])
                PS.release(bk)
                yield
                bk = PS.alloc()
                for h in range(4):
                    A("pe", "transpose", _c(PBb(bk)[:, h * 128:(h + 1) * 128], qkvT[:, 8 + h, bs], identb[:]),
                      r=[cqkv[8 + h], cIdb], w=pcells(bk))
                vt3 = PBb(bk)[:, 0:512].rearrange("p (h t) -> p h t", h=4)
                A("dve", "tensor_tensor", _c(out=vb[:], in0=vt3, in1=bc4(BETA, b), op=ALU.mult), r=pcells(bk) + c_ts, w=[cvb])
                PS.release(bk)
                yield
                bk = PS.alloc()
                for h in range(4):
                    A("pe", "matmul", _c(PB(bk)[:, h * 128:(h + 1) * 128], lhsT=kbg[:, h, :], rhs=Tinv[:, h, :], start=True, stop=True),
                      r=[ckbg, cTinv], w=pcells(bk))
                A("act", "activation", _c(out=nwT[:].rearrange("p h t -> p (h t)"), in_=PB(bk), func=AF.Copy, scale=-1.0), r=pcells(bk), w=[cnw])
                PS.release(bk)
                yield

            def gdn_recur(b, par):
                bs = slice(b * 128, (b + 1) * 128)
                Tinv, cTinv = TinvB[par], cTinvB[par]
                QKT, cQKT = QKTB[par], cQKTB[par]
                qdT, cqd = qdTB[par], cqdB[par]
                kdec, ckdec = kdecB[par], ckdecB[par]
                vb, cvb = vbB[par], cvbB[par]
                nwT, cnw = nwTB[par], cnwB[par]
                for c in range(2):
                    rr = slice(64 * c, 64 * c + 64)
                    bk = PS.alloc()
                    for h in range(4):
                        A("pe", "matmul", _c(PB(bk)[:, h * 128:(h + 1) * 128], lhsT=Tinv[rr, h, :], rhs=vb[rr, h, :], start=True, stop=False),
                          r=[cTinv, cvb], w=pcells(bk))
                        A("pe", "matmul", _c(PB(bk)[:, h * 128:(h + 1) * 128], lhsT=nwT[:, h, :], rhs=Sb[:, h, :], start=False, stop=True),
                          r=[cnw, cSb], w=pcells(bk))
                    A("act", "activation", _c(out=vnew[rr].rearrange("p h t -> p (h t)"), in_=PB(bk)[rr, :], func=AF.Copy),
                      r=pcells(bk), w=[cvnew])
                    PS.release(bk)
                    yield
                    bk = PS.alloc()
                    for h in range(4):
                        A("pe", "matmul", _c(PB(bk)[:, h * 128:(h + 1) * 128], lhsT=qdT[:, h, :], rhs=Sb[:, h, :], start=True, stop=False),
                          r=[cqd, cSb], w=pcells(bk))
                        A("pe", "matmul", _c(PB(bk)[:, h * 128:(h + 1) * 128], lhsT=QKT[rr, h, :], rhs=vnew[rr, h, :], start=False, stop=True),
                          r=[cQKT, cvnew], w=pcells(bk))
                    bk2 = PS.alloc()
                    for h in range(4):
                        A("pe", "matmul", _c(PB(bk2)[:, h * 128:(h + 1) * 128], lhsT=kdec[rr, h, :], rhs=vnew[rr, h, :], start=True, stop=True),
                          r=[ckdec, cvnew], w=pcells(bk2))
                    for h in range(4):
                        col = 48 + c * 16 + b * 4 + h
                        A("dve", "scalar_tensor_tensor", _c(out=Sf[:, h, :], in0=Sf[:, h, :], scalar=YEXP[:, col:col + 1],
                                                            in1=PB(bk2)[:, h * 128:(h + 1) * 128], op0=ALU.mult, op1=ALU.add),
                          r=pcells(bk2) + c_ts + [cSf[h]], w=[cSf[h]])
                    PS.release(bk2)
                    A("pool", "tensor_copy", _c(Sb[:], Sf[:]), r=cSf, w=[cSb])
                    A("act", "activation", _c(out=o_sb[rr].rearrange("p h t -> p (h t)"), in_=PB(bk)[rr, :], func=AF.Copy),
                      r=pcells(bk), w=[co])
                    PS.release(bk)
                    yield
                for h in range(4):
                    A("act", "activation", _c(out=junk[:, h, :], in_=o_sb[:, h, :], func=AF.Square, accum_out=gst[:, h:h + 1]),
                      r=[co], w=[cjk[h], cgsth[h]])
                rstd_from(gst[:, 0:4], gst[:, 4:8], 128, [cgst] + cgsth)
                A("dve", "tensor_tensor", _c(out=on[:], in0=o_sb[:], in1=gst[:, 4:8].unsqueeze(2).to_broadcast([128, 4, 128]), op=ALU.mult),
                  r=[co, cgst], w=[con])
                yield
                bk = PS.alloc()
                for h in range(4):
                    A("pe", "transpose", _c(PBb(bk)[:, h * 128:(h + 1) * 128], on[:, h, :], identb[:]), r=[con, cIdb], w=pcells(bk))
                ot3 = PBb(bk)[:, 0:512].rearrange("p (h t) -> p h t", h=4)
                A("dve", "tensor_tensor", _c(out=actT[:, 0:4, bs], in0=ot3, in1=szT[:, :, bs], op=ALU.mult),
                  r=pcells(bk) + [csz], w=cact[0:4])
                PS.release(bk)
                yield

            def swa_block(b):
                bs = slice(b * 128, (b + 1) * 128)
                bk = PS.alloc(4)
                for hq in range(8):
                    ph = slice(64 * (hq % 2), 64 * (hq % 2) + 64)
                    pos = (hq % 2) * 4 + hq // 2
                    A("pe", "matmul", _c(ps[:, bk * 512 + pos * 256: bk * 512 + (pos + 1) * 256], lhsT=qsT[ph, hq // 2, bs],
                                         rhs=ksT[ph, hq // 4, b * 128:b * 128 + 256], start=True, stop=True),
                      r=[cqs, cks], w=pcells(bk + pos // 2, 1))
                for q4 in range(4):
                    A("dve", "tensor_tensor", _c(out=sc_sb[:, 2 * q4:2 * q4 + 2, :].rearrange("p h s -> p (h s)"), in0=PB(bk + q4),
                                                 in1=biasb[:, 2 * q4:2 * q4 + 2, :].rearrange("p h s -> p (h s)"), op=ALU.add),
                      r=pcells(bk + q4, 1) + [cBias], w=[csc])
                PS.release(bk, 4)
                if first and b == 0:
                    A("pool", "memset", _c(sc_sb[:, :, 0:128], NEG), w=[csc])
                yield
                MX, NMX, SUMS, DEN = sst[:, 0:8], sst[:, 8:16], ssum[:, 0:8], sden[:, 0:8]
                A("dve", "reduce_max", _c(out=MX, in_=sc_sb[:], axis=AX.X), r=[csc], w=[csst])
                A("dve", "tensor_tensor", _c(out=MX, in0=MX, in1=small[:, 0:8], op=ALU.max), r=[csst, cC], w=[csst])
                A("dve", "tensor_scalar", _c(out=NMX, in0=MX, scalar1=-1.0, scalar2=None, op0=ALU.mult), r=[csst], w=[csst])
                yield
                for hq in range(8):
                    A("act", "activation", _c(out=pt[:, hq, :], in_=sc_sb[:, hq, :], func=AF.Exp, bias=sst[:, 8 + hq:9 + hq],
                                              accum_out=ssum[:, hq:hq + 1]), r=[csc, csst], w=[cpt[hq], csum[hq]])
                A("dve", "tensor_tensor", _c(out=DEN, in0=small[:, 0:8], in1=NMX, op=ALU.add), r=[csst, cC], w=[cden])
                A("act", "activation", _c(out=DEN, in_=DEN, func=AF.Exp), r=[cden], w=[cden])
                A("dve", "tensor_tensor", _c(out=DEN, in0=DEN, in1=SUMS, op=ALU.add), r=[cden] + csum, w=[cden])
                A("dve", "reciprocal", _c(DEN, DEN), r=[cden], w=[cden])
                yield
                bk = PS.alloc(2)
                for hq in range(8):
                    for kh in range(2):
                        j = hq * 2 + kh
                        A("pe", "transpose", _c(PBb(bk, 2)[:, j * 128:(j + 1) * 128], pt[:, hq, kh * 128:(kh + 1) * 128], identb[:]),
                          r=[cpt[hq], cIdb], w=pcells(bk + (j // 8), 1))
                A("act", "activation", _c(out=pT_sb[:, 0:8, :].rearrange("p j t -> p (j t)"), in_=PBb(bk, 2)[:, 0:1024], func=AF.Copy),
                  r=pcells(bk, 1), w=[cpT])
                A("dve", "tensor_copy", _c(pT_sb[:, 8:16, :].rearrange("p j t -> p (j t)"), PBb(bk, 2)[:, 1024:2048]), r=pcells(bk + 1, 1), w=[cpT2])
                PS.release(bk, 2)
                yield
                bk = PS.alloc()
                for hq in range(8):
                    kv = ((hq % 4) * 2 + hq // 4) // 4
                    A("pe", "matmul", _c(PB(bk)[:, hq * 64:(hq + 1) * 64], lhsT=pT_sb[:, hq * 2, :], rhs=vs[:, b, kv * 64:(kv + 1) * 64],
                                         start=True, stop=False), r=[cpT, cpT2, cvs[b]], w=pcells(bk))
                    A("pe", "matmul", _c(PB(bk)[:, hq * 64:(hq + 1) * 64], lhsT=pT_sb[:, hq * 2 + 1, :], rhs=vs[:, b + 1, kv * 64:(kv + 1) * 64],
                                         start=False, stop=True), r=[cpT, cpT2, cvs[b + 1]], w=pcells(bk))
                A("dve", "tensor_tensor", _c(out=os_sb[:], in0=PB(bk).rearrange("p (h d) -> p h d", h=8),
                                             in1=DEN.unsqueeze(2).to_broadcast([128, 8, 64]), op=ALU.mult), r=pcells(bk) + [cden], w=[cos])
                PS.release(bk)
                yield
                osf = os_sb[:].rearrange("p h d -> p (h d)")
                A("act", "activation", _c(out=osn[:], in_=osf, func=AF.Square, accum_out=sst[:, 32:33]), r=[cos], w=[cosn, csn])
                rstd_from(sst[:, 32:33], sst[:, 33:34], 512, [csn])
                A("act", "activation", _c(out=osn[:], in_=osf, func=AF.Copy, scale=sst[:, 33:34]), r=[cos, csn], w=[cosn])
                yield
                bk = PS.alloc()
                for j in range(4):
                    A("pe", "transpose", _c(PBb(bk)[:, j * 128:(j + 1) * 128], osn[:, j * 128:(j + 1) * 128], identb[:]), r=[cosn, cIdb], w=pcells(bk))
                A("act", "activation", _c(out=actT[:, 4:8, bs], in_=PBb(bk)[:, 0:512].rearrange("p (j t) -> p j t", j=4), func=AF.Copy),
                  r=pcells(bk), w=cact[4:8])
                PS.release(bk)
                yield

            nb = 4 if limit >= 6 else 1
            drive(inproj_gen(0, 9))
            drive(inproj_gen(9, 22), (seq(d0_gen(), gdn_chain(0, 0)), 1))
            tap("qkv", qkvT[:], cqkv, [128, 12, 512], BF16)
            for b in range(nb):
                nxt_chain = gdn_chain(b + 1, (b + 1) % 2) if b + 1 < nb else None
                if limit < 5:
                    drive(gdn_recur(b, b % 2), nxt_chain)
                elif nxt_chain is None:
                    drive(gdn_recur(b, b % 2), swa_block(b))
                else:
                    drive(seq(gdn_recur(b, b % 2), swa_block(b)), nxt_chain)
            if limit < 6:
                return bail(tok0)
            A("pool", "tensor_copy", _c(ksT[:, :, 0:128], ksT[:, :, 512:640]), r=[cks], w=[cks])
            A("pool", "tensor_copy", _c(vs[:, 0, :], vs[:, 4, :]), r=[cvs[4]], w=[cvs[0]])
            tap("cat", actT[:, 0:8, :], cact[0:8], [128, 8, 512], BF16)

            def proj_pass(kind, nk, wnorm):
                assert all(PS.free)
                for i in range(8):
                    PS.free[i] = False
                for k in range(nk):
                    slot, cs = wget(kind, k)
                    flat = slot[:].rearrange("p a b -> p (a b)")
                    for b in range(4):
                        for n in range(2):
                            A("pe", "matmul", _c(PB(b * 2 + n), lhsT=actT[:, k, b * 128:(b + 1) * 128],
                                                                                rhs=flat[:, n * 512:(n + 1) * 512], start=(k == 0), stop=(k == nk - 1)),
                              r=[cs, cact[k]], w=pcells(b * 2 + n))

            def epilogue(b, wnorm, after):
                hn, chn, stt, cstt = hnB[b], chnB[b], sttB[b], csttB[b]
                for n in range(2):
                    A("act", "activation", _c(out=hn[:, n * 512:(n + 1) * 512], in_=PB(b * 2 + n), func=AF.Square, accum_out=stt[:, 4 + n:5 + n]),
                      r=pcells(b * 2 + n, 1), w=[chn, cstt])
                yield
                A("dve", "tensor_tensor", _c(out=stt[:, 2:3], in0=stt[:, 4:5], in1=stt[:, 5:6], op=ALU.add), r=[cstt], w=[cstt])
                yield
                A("act", "activation", _c(out=stt[:, 3:4], in_=stt[:, 2:3], func=AF.Ln, bias=EPS, scale=1.0 / 1024), r=[cstt], w=[cstt])
                yield
                A("act", "activation", _c(out=stt[:, 3:4], in_=stt[:, 3:4], func=AF.Exp, scale=-0.5), r=[cstt], w=[cstt])
                yield
                for n in range(2):
                    A("dve", "scalar_tensor_tensor", _c(out=PB(b * 2 + n), in0=PB(b * 2 + n), scalar=stt[:, 3:4],
                                                        in1=wnorm[:, n * 512:(n + 1) * 512], op0=ALU.mult, op1=ALU.mult),
                      r=pcells(b * 2 + n, 1) + [cstt, cC], w=pcells(b * 2 + n, 1))
                yield
                for n in range(2):
                    A("dve", "tensor_tensor", _c(out=xb[b][:, n * 512:(n + 1) * 512], in0=xb[b][:, n * 512:(n + 1) * 512], in1=PB(b * 2 + n), op=ALU.add),
                      r=pcells(b * 2 + n, 1) + [cx[b]], w=[cx[b]])
                PS.release(b * 2, 2)
                yield
                if after is not None:
                    yield from after(b)

            proj_pass("wo", 8, wpm)
            drive(*[epilogue(b, wpm, norm_transpose) for b in range(4)])
            tap("x1", xb[0][:], [cx[0]], [128, 1024])

            if limit < 7:
                return bail(tok0)
            for f in range(22):
                slot, cs = wget("ffn", f)
                gp = f % 2
                bg = PS.alloc()
                for kc in range(8):
                    A("pe", "matmul", _c(PB(bg), lhsT=slot[:, kc, :], rhs=hT[:, kc, :], start=(kc == 0), stop=(kc == 7)),
                      r=[cs] + chT, w=pcells(bg))
                bu = PS.alloc()
                for kc in range(8):
                    A("pe", "matmul", _c(PB(bu), lhsT=slot[:, 8 + kc, :], rhs=hT[:, kc, :], start=(kc == 0), stop=(kc == 7)),
                      r=[cs] + chT, w=pcells(bu))
                A("pool", "tensor_copy", _c(gpre[gp][:, 0:2], ghf[:, f, :]), r=[cghf], w=[cgpreh[gp]])
                A("act", "activation", _c(out=gpre[gp][:, 2:514], in_=PB(bg), func=AF.Copy), r=pcells(bg), w=[cgpre[gp]])
                PS.release(bg)
                A("pool", "tensor_copy", _c(ghf[:, f, :], gpre[gp][:, 512:514]), r=[cgpre[gp]], w=[cghf])
                bc = PS.alloc()
                for i in range(3):
                    A("pe", "matmul", _c(PB(bc), lhsT=slot[:, 16 + i, :], rhs=gpre[gp][:, i:i + 512], start=(i == 0), stop=(i == 2)),
                      r=[cs, cgpre[gp], cgpreh[gp]], w=pcells(bc))
                A("act", "activation", _c(out=ge[gp][:], in_=PB(bc), func=AF.Gelu_apprx_tanh, bias=fcb[:, f:f + 1]),
                  r=pcells(bc) + [cC], w=[cge[gp]])
                PS.release(bc)
                A("dve", "tensor_tensor", _c(out=actT[:, f, :], in0=PB(bu), in1=ge[gp][:], op=ALU.mult), r=pcells(bu) + [cge[gp]], w=[cact[f]])
                PS.release(bu)
            tap("act", actT[:], cact, [128, 22, 512], BF16)
            proj_pass("wd", 22, wpf)

            def store_out(b):
                A("sp", "dma_start", _c(out=y_d[tok0 + b * 128: tok0 + (b + 1) * 128, :], in_=xb[b][:]), r=[cx[b]], dma=cx[b], store=True)
                yield

            drive(*[epilogue(b, wpf, store_out) for b in range(4)])

        for s in range(NSEQ):
            reset_state()
            for mc in range(NMC):
                macro_chunk(s, mc)
        stats = S.emit()
    return nc, stats, tap_out


def _host_inputs(inp):
    f = lambda k: np.ascontiguousarray(np.asarray(inp[k], dtype=np.float32))
    pk = lambda v: np.ascontiguousarray(v.reshape(-1, 128).T)
    gdn_norm = f("gdn_norm_w")[0]
    swa_norm = f("swa_norm_w")[0]
    hq_of = [(p % 4) * 2 + p // 4 for p in range(8)]
    swa_perm = np.concatenate([swa_norm[h * 64:(h + 1) * 64] for h in hq_of])
    wos = np.concatenate([np.tile(gdn_norm, 4), swa_perm])
    vec8 = np.concatenate([pk(f("pre_mix_norm_w")[0]), pk(f("pre_ffn_norm_w")[0]), pk(wos)], axis=1)
    gc = f("gdn_conv_w")[0]
    gcw = np.ascontiguousarray(gc.reshape(4, 12, 128).transpose(2, 1, 0).reshape(128, 48))
    fc = f("ffn_conv_w")[0]
    fcw = np.ascontiguousarray(fc.reshape(3, 22, 128).transpose(2, 1, 0).reshape(128, 66))
    fcb = pk(f("ffn_conv_b")[0])
    bc = lambda v: np.ascontiguousarray(np.broadcast_to(v[None, :], (128, v.shape[0])))
    small = np.concatenate([bc(f("swa_sinks")[0][hq_of]), bc(f("gdn_dt_bias")[0]), bc(f("gdn_a_log")[0])], axis=1)
    tbl = f("rel_bias_table")
    relb = np.ascontiguousarray(tbl[_bucket_table()].transpose(0, 2, 1)[:, hq_of, :].reshape(128, 2048))
    shared = {
        "w_in": f("w_in")[0], "w_out": f("w_out")[0], "w_gate": f("w_gate")[0], "w_up": f("w_up")[0], "w_down": f("w_down")[0],
        "vec8": vec8, "gcw": gcw, "fcw": fcw, "fcb": fcb, "wpm": bc(f("post_mix_norm_w")[0]), "wpf": bc(f("post_ffn_norm_w")[0]),
        "small": small, "relb": relb, "cst": _const_table(),
    }
    return shared


_CACHE = {}


def kernel(**inputs):
    x = np.asarray(inputs["x"], dtype=np.float32)
    B, Tn, Dm = x.shape
    per = B // N_CORES
    shared = _host_inputs(inputs)
    key = (per, Tn)
    if key not in _CACHE:
        _CACHE[key] = build_program(NSEQ=per, NMC=Tn // 512)[0]
    nc = _CACHE[key]
    in_maps = []
    for c in range(N_CORES):
        m = dict(shared)
        m["x"] = np.ascontiguousarray(x[c * per:(c + 1) * per].reshape(per * Tn, Dm))
        in_maps.append(m)
    res = run_bass_kernel_spmd(nc, in_maps, core_ids=list(range(N_CORES)))
    out = np.concatenate([np.asarray(r["y"]).reshape(per, Tn, Dm) for r in res.results], axis=0)
    return out.astype(np.float32)
```
